# Optimizing a Trainium2 kernel written in Bass

```python
import jax, jax.numpy as jnp
from jax import lax
import numpy as np

D_MODEL = 4096
BATCH = 4
SEQ = 2048
DEPTH = 1

HEAD_DIM = 128
N_HEADS = D_MODEL // (2 * HEAD_DIM)
N_KV_HEADS = N_HEADS // 4
GQA_GROUP = N_HEADS // N_KV_HEADS
ATTN_WIDTH = N_HEADS * HEAD_DIM
KV_WIDTH = N_KV_HEADS * HEAD_DIM
WINDOW = 128
BLOCK = 128
ROPE_THETA = 10000.0
POOL_WIDTH = D_MODEL - ATTN_WIDTH
POOL_WINDOWS = (2, 4, 8, 16)
N_POOL_GROUPS = len(POOL_WINDOWS)
POOL_GROUP_WIDTH = POOL_WIDTH // N_POOL_GROUPS
MIX_WIDTH = ATTN_WIDTH + POOL_WIDTH
IN_PROJ_WIDTH = ATTN_WIDTH + 2 * KV_WIDTH + POOL_WIDTH
D_FF = ((8 * D_MODEL // 3 + 255) // 256) * 256
FFN_RES_WEIGHT = 0.5
RMS_EPS = 1e-6

kernel_name = "hymba_swa_sink_pool_macaron"


def rmsnorm(x, g):
    xf = x.astype(jnp.float32)
    y = xf * lax.rsqrt(jnp.mean(xf * xf, axis=-1, keepdims=True) + RMS_EPS)
    return (y * g.astype(jnp.float32)).astype(x.dtype)


def swiglu(h, w_gate, w_up, w_down):
    return (jax.nn.silu(h @ w_gate) * (h @ w_up)) @ w_down


def rope(t):
    s = t.shape[1]
    pos = jnp.arange(s, dtype=jnp.float32)
    inv_freq = ROPE_THETA ** (-jnp.arange(0, HEAD_DIM, 2, dtype=jnp.float32) / HEAD_DIM)
    ang = pos[:, None] * inv_freq[None, :]
    cos = jnp.cos(ang)[None, :, None, :]
    sin = jnp.sin(ang)[None, :, None, :]
    tf = t.astype(jnp.float32)
    t1, t2 = tf[..., : HEAD_DIM // 2], tf[..., HEAD_DIM // 2:]
    out = jnp.concatenate([t1 * cos - t2 * sin, t2 * cos + t1 * sin], axis=-1)
    return out.astype(t.dtype)


def sliding_window_attention(q, k, v, sinks):
    b, s = q.shape[0], q.shape[1]
    nb = s // BLOCK
    qb = q.reshape(b, nb, BLOCK, N_KV_HEADS, GQA_GROUP, HEAD_DIM)

    def with_prev(t):
        t = t.reshape(b, nb, BLOCK, N_KV_HEADS, HEAD_DIM)
        prev = jnp.concatenate([jnp.zeros_like(t[:, :1]), t[:, :-1]], axis=1)
        return jnp.concatenate([prev, t], axis=2)

    kb, vb = with_prev(k), with_prev(v)
    scale = HEAD_DIM ** -0.5
    sc = jnp.einsum('bnqhgd,bnkhd->bhgnqk', qb, kb,
                    preferred_element_type=jnp.float32) * scale
    qi = jnp.arange(BLOCK)[:, None]
    ki = jnp.arange(2 * BLOCK)[None, :]
    diff = qi + BLOCK - ki
    band = (diff >= 0) & (diff < WINDOW)
    blk = jnp.arange(nb)[:, None, None]
    valid = band[None] & ((blk > 0) | (ki >= BLOCK)[None])
    sc = jnp.where(valid[None, None, None], sc, -jnp.inf)
    sink = sinks.astype(jnp.float32).reshape(N_KV_HEADS, GQA_GROUP)[None, :, :, None, None, None]
    m = jnp.maximum(jnp.max(sc, axis=-1, keepdims=True), sink)
    e = jnp.exp(sc - m)
    denom = jnp.sum(e, axis=-1, keepdims=True) + jnp.exp(sink - m)
    p = (e / denom).astype(vb.dtype)
    out = jnp.einsum('bhgnqk,bnkhd->bnqhgd', p, vb)
    return out.reshape(b, s, ATTN_WIDTH)


def multiscale_pool(p, pool_w, pool_scale):
    b, s = p.shape[0], p.shape[1]
    pg = p.reshape(b, s, N_POOL_GROUPS, POOL_GROUP_WIDTH).astype(jnp.float32)
    c = jnp.cumsum(pg, axis=1)
    c0 = jnp.concatenate([jnp.zeros_like(c[:, :1]), c], axis=1)
    t = jnp.arange(s)
    outs = []
    for g, w in enumerate(POOL_WINDOWS):
        cg = c0[:, :, g, :]
        lo = jnp.maximum(t + 1 - w, 0)
        win_sum = cg[:, 1:] - jnp.take(cg, lo, axis=1)
        count = jnp.minimum(t + 1, w).astype(jnp.float32)[None, :, None]
        outs.append(win_sum / count)
    pooled = jnp.stack(outs, axis=2)
    y = (pooled - pg).astype(p.dtype)
    y = jnp.einsum('bsgc,gcd->bsgd', y, pool_w).reshape(b, s, POOL_WIDTH)
    return y * pool_scale


def setup_inputs(seed: int = 0) -> dict:
    key = jax.random.key(seed)
    ks = jax.random.split(key, 20)
    f32 = jnp.float32

    def w(k, shape, fan_in):
        return jax.random.normal(k, shape, f32) * fan_in ** -0.5

    def gain(k, shape):
        return 1.0 + 0.05 * jax.random.normal(k, shape, f32)

    L = DEPTH
    return {
        "x": jax.random.normal(ks[0], (BATCH, SEQ, D_MODEL), f32),
        "ffn1_pre_g": gain(ks[1], (L, D_MODEL)),
        "ffn1_w_gate": w(ks[2], (L, D_MODEL, D_FF), D_MODEL),
        "ffn1_w_up": w(ks[3], (L, D_MODEL, D_FF), D_MODEL),
        "ffn1_w_down": w(ks[4], (L, D_FF, D_MODEL), D_FF),
        "ffn1_post_g": gain(ks[5], (L, D_MODEL)),
        "mix_pre_g": gain(ks[6], (L, D_MODEL)),
        "w_in": w(ks[7], (L, D_MODEL, IN_PROJ_WIDTH), D_MODEL),
        "attn_sinks": jax.random.normal(ks[8], (L, N_HEADS), f32),
        "pool_w": w(ks[9], (L, N_POOL_GROUPS, POOL_GROUP_WIDTH, POOL_GROUP_WIDTH), POOL_GROUP_WIDTH),
        "pool_scale": gain(ks[10], (L, POOL_WIDTH)),
        "w_out": w(ks[11], (L, MIX_WIDTH, D_MODEL), MIX_WIDTH),
        "mix_post_g": gain(ks[12], (L, D_MODEL)),
        "ffn2_pre_g": gain(ks[13], (L, D_MODEL)),
        "ffn2_w_gate": w(ks[14], (L, D_MODEL, D_FF), D_MODEL),
        "ffn2_w_up": w(ks[15], (L, D_MODEL, D_FF), D_MODEL),
        "ffn2_w_down": w(ks[16], (L, D_FF, D_MODEL), D_FF),
        "ffn2_post_g": gain(ks[17], (L, D_MODEL)),
    }


def reference(x, ffn1_pre_g, ffn1_w_gate, ffn1_w_up, ffn1_w_down, ffn1_post_g,
              mix_pre_g, w_in, attn_sinks, pool_w, pool_scale, w_out, mix_post_g,
              ffn2_pre_g, ffn2_w_gate, ffn2_w_up, ffn2_w_down, ffn2_post_g):
    b, s = x.shape[0], x.shape[1]
    for l in range(DEPTH):
        h = swiglu(rmsnorm(x, ffn1_pre_g[l]), ffn1_w_gate[l], ffn1_w_up[l], ffn1_w_down[l])
        x = x + FFN_RES_WEIGHT * rmsnorm(h, ffn1_post_g[l])

        h = rmsnorm(x, mix_pre_g[l])
        proj = h @ w_in[l]
        q = proj[..., :ATTN_WIDTH]
        k = proj[..., ATTN_WIDTH:ATTN_WIDTH + KV_WIDTH]
        v = proj[..., ATTN_WIDTH + KV_WIDTH:ATTN_WIDTH + 2 * KV_WIDTH]
        p = proj[..., ATTN_WIDTH + 2 * KV_WIDTH:]
        q = rope(q.reshape(b, s, N_HEADS, HEAD_DIM))
        k = rope(k.reshape(b, s, N_KV_HEADS, HEAD_DIM))
        v = v.reshape(b, s, N_KV_HEADS, HEAD_DIM)
        a = sliding_window_attention(q, k, v, attn_sinks[l])
        pm = multiscale_pool(p, pool_w[l], pool_scale[l])
        y = jnp.concatenate([a, pm], axis=-1) @ w_out[l]
        x = x + rmsnorm(y, mix_post_g[l])

        h = swiglu(rmsnorm(x, ffn2_pre_g[l]), ffn2_w_gate[l], ffn2_w_up[l], ffn2_w_down[l])
        x = x + FFN_RES_WEIGHT * rmsnorm(h, ffn2_post_g[l])
    return x
```

```python
import contextlib
import numpy as np
import concourse.bass as bass
import concourse.mybir as mybir
from concourse.bass_utils import run_bass_kernel_spmd

F32 = mybir.dt.float32
BF16 = mybir.dt.bfloat16
AF = mybir.ActivationFunctionType
ALU = mybir.AluOpType
AX = mybir.AxisListType

RMS_EPS = 1e-6
ROPE_THETA = 10000.0
NEG = -30000.0


class Cfg:
    def __init__(self, D=4096, SEQ=2048, BATCH=4, NCORE=8, T=384, NS=11, NB=4,
                 cast_pattern=("act", "dve", "act", "dve", "pool", "act", "dve", "dve")):
        self.D = D
        self.DC = D // 128
        self.FF = ((8 * D // 3 + 255) // 256) * 256
        self.NFB = self.FF // 256
        self.NFB0 = (self.NFB + 1) // 2
        self.NH = D // 256
        self.NKV = self.NH // 4
        self.AW = self.NH * 128
        self.KVW = self.NKV * 128
        self.PW = D - self.AW
        self.PGW = self.PW // 4
        self.PT = self.PW // 128
        self.PGT = self.PGW // 128
        self.INW = self.AW + 2 * self.KVW + self.PW
        self.DG = D // 512
        self.SEQ, self.BATCH, self.NCORE = SEQ, BATCH, NCORE
        self.CPS = NCORE // BATCH
        self.TOK = SEQ // self.CPS
        self.HALO = 128
        self.T = T
        self.TOKH = self.TOK + self.HALO
        self.NT = self.TOKH // T
        assert self.NT * T == self.TOKH and T % 128 == 0
        self.NS, self.NB = NS, NB
        self.cast_pattern = cast_pattern
        cgs = []
        nq, npg = self.AW // 512, self.PW // 512
        for i in range(nq):
            cgs.append(("q", i * 512, 512, i))
        cgs.append(("k", self.AW, self.KVW, 0))
        cgs.append(("v", self.AW + self.KVW, self.KVW, 0))
        for i in reversed(range(npg)):
            cgs.append(("p", self.AW + 2 * self.KVW + i * 512, 512, i))
        self.cgs = cgs
        self.NBLK = (2 * (self.NFB * self.DC + self.DG * 2 * self.NFB)
                     + len(cgs) * self.DC + 4 * self.PGT + self.DG * self.DC)


def build_wstream(cfg, W):
    DC, NFB, NFB0, DG = cfg.DC, cfg.NFB, cfg.NFB0, cfg.DG
    out = np.zeros((cfg.NBLK, 128, 512), np.float32)
    pos = 0

    def ffn(wg, wu, wd):
        nonlocal pos
        g = wg.reshape(DC, 128, NFB, 256).transpose(2, 0, 1, 3)
        u = wu.reshape(DC, 128, NFB, 256).transpose(2, 0, 1, 3)
        d = wd.reshape(2 * NFB, 128, DG, 512).transpose(2, 0, 1, 3)
        for (fb0, fb1) in ((0, NFB0), (NFB0, NFB)):
            n = (fb1 - fb0) * DC
            blk = out[pos:pos + n].reshape(fb1 - fb0, DC, 128, 512)
            blk[..., 0:256] = g[fb0:fb1]
            blk[..., 256:512] = u[fb0:fb1]
            pos += n
            nfc = 2 * (fb1 - fb0)
            n = DG * nfc
            out[pos:pos + n] = d[:, 2 * fb0:2 * fb1].reshape(n, 128, 512)
            pos += n

    ffn(W["ffn1_w_gate"], W["ffn1_w_up"], W["ffn1_w_down"])
    win = W["w_in"]
    for (_kind, s, w, _i) in cfg.cgs:
        out[pos:pos + DC, :, 0:w] = win[:, s:s + w].reshape(DC, 128, w)
        pos += DC
    pw = W["pool_w"]
    for g in range(4):
        out[pos:pos + cfg.PGT, :, 0:cfg.PGW] = pw[g].reshape(cfg.PGT, 128, cfg.PGW)
        pos += cfg.PGT
    wo = W["w_out"].reshape(DC, 128, DG, 512).transpose(2, 0, 1, 3)
    out[pos:pos + DG * DC] = wo.reshape(DG * DC, 128, 512)
    pos += DG * DC
    ffn(W["ffn2_w_gate"], W["ffn2_w_up"], W["ffn2_w_down"])
    assert pos == cfg.NBLK, (pos, cfg.NBLK)
    return out


def build_core_inputs(cfg, inputs):
    f = np.float32
    DC = cfg.DC
    W = {k: np.asarray(v, f)[0] for k, v in inputs.items() if k != "x"}
    x = np.asarray(inputs["x"], f)
    wst = build_wstream(cfg, W)
    gnames = ["ffn1_pre_g", "ffn1_post_g", "mix_pre_g", "mix_post_g", "ffn2_pre_g", "ffn2_post_g"]
    gains = np.stack([W[n].reshape(DC, 128).T for n in gnames], axis=1)
    gains = np.ascontiguousarray(gains, f)
    sinks = np.ascontiguousarray(np.broadcast_to(W["attn_sinks"][None, :], (128, cfg.NH)), f)
    psc = np.ascontiguousarray(W["pool_scale"].reshape(cfg.PT, 128).T, f)
    ident = np.eye(128, dtype=f)
    inv_freq = (f(ROPE_THETA) ** (-np.arange(0, 128, 2, dtype=f) / f(128))).astype(f)
    qi = np.arange(128)[:, None]
    ki = np.arange(256)[None, :]
    diff = qi + 128 - ki
    band = (diff >= 0) & (diff < 128)
    in_maps = []
    for core in range(cfg.NCORE):
        b, s = divmod(core, cfg.CPS)
        s0 = s * cfg.TOK
        xT = np.zeros((128, DC, cfg.TOKH), f)
        lo = s0 - cfg.HALO
        src_lo = max(lo, 0)
        seg = x[b, src_lo:s0 + cfg.TOK]
        xT[:, :, src_lo - lo:] = seg.T.reshape(DC, 128, -1).transpose(1, 0, 2)
        pos = (np.arange(cfg.TOKH) + lo).astype(f)
        ang = (pos[:, None] * inv_freq[None, :]).astype(f)
        c = np.cos(ang).astype(f).T
        sn = np.sin(ang).astype(f).T
        rope = np.stack([np.concatenate([c, c], 0), np.concatenate([sn, -sn], 0)], axis=1)
        rope = np.ascontiguousarray(rope, f)
        m0 = band.copy()
        if s0 == 0:
            m0 &= (ki >= 128)
        masks = np.stack([np.where(m0, 0.0, NEG), np.where(band, 0.0, NEG)], axis=1).astype(f)
        t = np.arange(256)
        invc = np.zeros((128, 4, 256), f)
        for g, w in enumerate((2, 4, 8, 16)):
            cnt = np.minimum(t + 1, w) if s0 == 0 else np.full(256, w)
            invc[:, g, :] = (1.0 / cnt.astype(f)).astype(f)[None, :]
        in_maps.append({"xT": xT, "wst": wst, "gains": gains, "sinks": sinks, "psc": psc,
                        "ident": ident, "rope": rope, "masks": masks, "invc": invc})
    return in_maps


def assemble_output(cfg, results):
    out = np.zeros((cfg.BATCH, cfg.SEQ, cfg.D), np.float32)
    for core in range(cfg.NCORE):
        b, s = divmod(core, cfg.CPS)
        oT = np.asarray(results[core]["outT"])
        out[b, s * cfg.TOK:(s + 1) * cfg.TOK] = oT.transpose(2, 1, 0).reshape(cfg.TOK, cfg.D)
    return out


class Ev:
    __slots__ = ("sem", "val", "eng")

    def __init__(self, sem, val, eng):
        self.sem, self.val, self.eng = sem, val, eng


class Sched:
    ENG = ("sync", "act", "dve", "pool", "pe")

    def __init__(self):
        self.ops = {e: [] for e in self.ENG}
        self.cnt = {e: 0 for e in self.ENG}
        self.dma_cnt = {}

    def op(self, eng, fn, waits=(), signal=True):
        ws = [w for w in waits if w is not None]
        ev = None
        if signal:
            self.cnt[eng] += 1
            ev = Ev(("eng", eng), self.cnt[eng], eng)
        self.ops[eng].append((fn, ws, signal, None))
        return ev

    def dma(self, fn, semname, waits=()):
        ws = [w for w in waits if w is not None]
        self.dma_cnt[semname] = self.dma_cnt.get(semname, 0) + 16
        ev = Ev(("dma", semname), self.dma_cnt[semname], "dma")
        self.ops["sync"].append((fn, ws, False, semname))
        return ev

    def wait_only(self, eng, waits):
        self.ops[eng].append((None, [w for w in waits if w is not None], False, None))


def build_program(cfg, dbg=None):
    DC, T, NS, NB = cfg.DC, cfg.T, cfg.NS, cfg.NB
    NH, NKV, PT, PGT, DG = cfg.NH, cfg.NKV, cfg.PT, cfg.PGT, cfg.DG
    NFB, NFB0 = cfg.NFB, cfg.NFB0
    NBLK = cfg.NBLK
    D = cfg.D
    nc = bass.Bass("TRN2", target_bir_lowering=False)
    xT = nc.dram_tensor("xT", [128, DC, cfg.TOKH], F32, kind="ExternalInput").ap()
    wst = nc.dram_tensor("wst", [NBLK, 128, 512], F32, kind="ExternalInput").ap()
    gains_d = nc.dram_tensor("gains", [128, 6, DC], F32, kind="ExternalInput").ap()
    sinks_d = nc.dram_tensor("sinks", [128, NH], F32, kind="ExternalInput").ap()
    psc_d = nc.dram_tensor("psc", [128, PT], F32, kind="ExternalInput").ap()
    ident_d = nc.dram_tensor("ident", [128, 128], F32, kind="ExternalInput").ap()
    rope_d = nc.dram_tensor("rope", [128, 2, cfg.TOKH], F32, kind="ExternalInput").ap()
    masks_d = nc.dram_tensor("masks", [128, 2, 256], F32, kind="ExternalInput").ap()
    invc_d = nc.dram_tensor("invc", [128, 4, 256], F32, kind="ExternalInput").ap()
    outT = nc.dram_tensor("outT", [128, DC, cfg.TOK], F32, kind="ExternalOutput").ap()

    S = Sched()
    es = contextlib.ExitStack()

    def sb(name, n, dt):
        return es.enter_context(nc.sbuf_tensor(name, [128, n], dt))

    def v3(t, off, a, b):
        return t[:, off:off + a * b].rearrange("p (a b) -> p a b", b=b)

    RA = sb("RA", DC * T, F32)
    RB = sb("RB", DC * T, F32)
    RC = sb("RC", DC * T, BF16)
    nd = max(2 * NFB0, 2 * NH) * T
    RD = sb("RD", nd, BF16)
    KRt = sb("KR", NKV * (128 + T), BF16)
    VTt = sb("VT", 4 * cfg.KVW, BF16)
    PHt = sb("PH", PT * 16, F32)
    STG = sb("STG", NS * 512, F32)
    WBt = sb("WB", NB * 512, BF16)
    SGt = sb("SG", 2 * T, F32)
    NRM = sb("NRM", 4 * T, F32)
    GAINS = sb("GAINS", 6 * DC, F32)
    SINKS = sb("SINKS", NH, F32)
    PSC = sb("PSC", PT, F32)
    IDENT = sb("IDENT", 128, BF16)
    ONES = sb("ONES", 128, F32)
    MASKS = sb("MASKS", 512, F32)
    ROPE = sb("ROPE", 2 * T, F32)

    X = v3(RA, 0, DC, T)
    Y = v3(RB, 0, DC, T)
    P = v3(RB, 0, PT, 16 + T)
    H = v3(RC, 0, DC, T)
    PD = v3(RC, 0, PT, T)
    PM = v3(RC, PT * T, PT, T)
    ACTB = v3(RD, 0, 2 * NFB0, T)
    QR = v3(RD, 0, NH, T)
    AO = v3(RD, NH * T, NH, T)
    KR = v3(KRt, 0, NKV, 128 + T)
    VT = v3(VTt, 0, 4, cfg.KVW)
    PH = v3(PHt, 0, PT, 16)
    SG = v3(SGt, 0, 2, T)
    SQ = v3(NRM, 0, 2, T)
    RS = NRM[:, 2 * T:3 * T]
    RSTD = NRM[:, 3 * T:4 * T]
    COS = ROPE[:, 0:T]
    SIN = ROPE[:, T:2 * T]
    MASK = v3(MASKS, 0, 2, 256)

    NSET = 4
    need = 5 * T + NSET * (256 + 128 + 128 + 16) + 2 * (T + 16) + 1024
    p_end = PT * (16 + T)
    if DC * T - p_end >= need:
        SCR, so = RB, p_end
    else:
        SCR, so = sb("SCR", need, F32), 0
    SCRB = SCR.bitcast(BF16)

    def carve(n):
        nonlocal so
        o = so
        so += n
        return o

    o = carve(4 * T); QF = v3(SCR, o, 4, T)
    o = carve(T); T1 = SCR[:, o:o + T]
    o = carve(NSET * 256); SS = v3(SCR, o, NSET, 256)
    o = carve(NSET * 128); PB = v3(SCRB, 2 * o, NSET, 256)
    o = carve(NSET * 128); PTT = v3(SCRB, 2 * o, NSET, 256)
    o = carve(NSET * 16); VEC = v3(SCR, o, NSET, 16)
    o = carve(T + 16); S1 = SCR[:, o:o + T + 16]
    o = carve(T + 16); S2 = SCR[:, o:o + T + 16]
    o = carve(1024); INVC = v3(SCR, o, 4, 256)

    PS = es.enter_context(nc.psum_tensor("PS", [128, 8, 512], F32))
    PSB = PS.bitcast(BF16)

    TOTAL = NBLK * cfg.NT
    st = {"dma": 0, "cast": 0, "pe": 0}
    dma_ev, cast_ev, pe_ev = {}, {}, {}
    cast_engs = cfg.cast_pattern

    def copy_fn(eng, out, in_):
        if eng == "act":
            return lambda e: e.activation(out=out, in_=in_, func=AF.Copy)
        return lambda e: e.tensor_copy(out=out, in_=in_)

    def pump():
        while True:
            prog = False
            j = st["dma"]
            if j < TOTAL and j - NS < st["cast"]:
                slot = j % NS
                dst = STG[:, slot * 512:(slot + 1) * 512]
                src = wst[j % NBLK]
                dma_ev[j] = S.dma(lambda e, d=dst, s_=src: e.dma_start(out=d, in_=s_),
                                  "stg%d" % slot, [cast_ev.get(j - NS)])
                st["dma"] += 1
                prog = True
            j = st["cast"]
            if j < st["dma"] and j - NB < st["pe"]:
                eng = cast_engs[j % len(cast_engs)]
                if eng == "pool" and st.get("nopool"):
                    eng = "act" if (j // len(cast_engs)) % 2 == 0 else "dve"
                src = STG[:, (j % NS) * 512:(j % NS + 1) * 512]
                dst = WBt[:, (j % NB) * 512:(j % NB + 1) * 512]
                cast_ev[j] = S.op(eng, copy_fn(eng, dst, src), [dma_ev[j], pe_ev.get(j - NB)])
                st["cast"] += 1
                prog = True
            if not prog:
                break

    def wblock(mk_ops, extra_waits=()):
        pump()
        j = st["pe"]
        assert j < st["cast"], "weight pipeline underflow"
        wb = WBt[:, (j % NB) * 512:(j % NB + 1) * 512]
        fns = mk_ops(wb)
        ev = None
        for i, fn in enumerate(fns):
            w = ([cast_ev[j]] + list(extra_waits)) if i == 0 else ()
            ev = S.op("pe", fn, w, signal=(i == len(fns) - 1))
        pe_ev[j] = ev
        st["pe"] += 1
        for d_ in (cast_ev, dma_ev, pe_ev):
            d_.pop(j - 4 * (NS + NB), None)
        return ev

    bank_free = [[] for _ in range(8)]
    bank_rr = [0]

    def alloc_banks(n):
        bs = []
        for _ in range(n):
            bs.append(bank_rr[0])
            bank_rr[0] = (bank_rr[0] + 1) % 8
        return bs

    def take_free(b):
        ev = bank_free[b]
        bank_free[b] = []
        return ev

    cev = []
    cev.append(S.dma(lambda e: e.dma_start(out=GAINS[:, :].rearrange("p (a b) -> p a b", b=DC), in_=gains_d), "cst"))
    cev.append(S.dma(lambda e: e.dma_start(out=SINKS[:, :], in_=sinks_d), "cst"))
    cev.append(S.dma(lambda e: e.dma_start(out=PSC[:, :], in_=psc_d), "cst"))
    cev.append(S.dma(lambda e: e.dma_start(out=MASK, in_=masks_d), "cst"))
    cev.append(S.dma(lambda e: e.dma_start(out=SGt[:, 0:128], in_=ident_d), "cst"))
    c_all = cev[-1]
    ev_ident = S.op("dve", lambda e: e.tensor_copy(out=IDENT[:, :], in_=SGt[:, 0:128]), [c_all])
    ev_ones = S.op("pool", lambda e: e.memset(ONES[:, :], 1.0), [])
    const_evs = [c_all, ev_ident, ev_ones]

    def gain(n, c):
        return GAINS[:, n * DC + c:n * DC + c + 1]

    state = {"x_ready": [], "sq_free": [None, None], "misc": []}

    def stats(src, c0, waits, chunk_waits=None, wgt=1.0, alt=False):
        b = alloc_banks(1)[0]
        bfree = take_free(b)
        last = None
        for c in range(DC):
            i = c % 2
            cw = list(chunk_waits[c]) if chunk_waits is not None else []
            if alt and c % 2 == 1:
                ev_sq = S.op("dve", lambda e, c=c, i=i: e.tensor_tensor(
                    out=SQ[:, i, c0:T], in0=src[:, c, c0:T], in1=src[:, c, c0:T], op=ALU.mult),
                    list(waits) + cw + [state["sq_free"][i]])
            else:
                ev_sq = S.op("act", lambda e, c=c, i=i: e.activation(
                    out=SQ[:, i, c0:T], in_=src[:, c, c0:T], func=AF.Square),
                    list(waits) + cw + [state["sq_free"][i]])
            last = S.op("pe", lambda e, c=c, i=i, b=b: e.matmul(
                PS[:, b, c0:T], ONES[:, :], SQ[:, i, c0:T], start=(c == 0), stop=(c == DC - 1)),
                [ev_sq, ev_ones] + (bfree if c == 0 else []))
            state["sq_free"][i] = last
        ev_rs = S.op("act", lambda e, b=b: e.activation(
            out=RS[:, c0:T], in_=PS[:, b, c0:T], func=AF.Sqrt, bias=EPSB[:, 0:1], scale=1.0 / D), [last])
        bank_free[b] = [ev_rs]
        ev_rstd = S.op("dve", lambda e: e.reciprocal(out=RSTD[:, c0:T], in_=RS[:, c0:T]), [ev_rs])
        if wgt != 1.0:
            ev_rstd = S.op("dve", lambda e: e.tensor_scalar(
                out=RSTD[:, c0:T], in0=RSTD[:, c0:T], scalar1=float(wgt), scalar2=None, op0=ALU.mult), [ev_rstd])
        return ev_rstd

    EPSB = sb("EPSB", 1, F32)
    ev_eps = S.op("pool", lambda e: e.memset(EPSB[:, :], RMS_EPS), [])
    const_evs.append(ev_eps)

    def prenorm(n, c0, waits):
        waits = list(waits)
        if len(waits) == DC:
            cw = [[w] for w in waits]
            ev_rstd = stats(X, c0, const_evs, cw)
        else:
            cw = [waits] * DC
            ev_rstd = stats(X, c0, waits + const_evs)
        evs = []
        for c in range(DC):
            evs.append(S.op("dve", lambda e, c=c: e.scalar_tensor_tensor(
                out=H[:, c, c0:T], in0=X[:, c, c0:T], scalar=gain(n, c), in1=RSTD[:, c0:T],
                op0=ALU.mult, op1=ALU.mult), [ev_rstd] + cw[c]))
        return evs

    def postnorm(n, c0, wgt, yev):
        ev_rstd = stats(Y, c0, [], [[w] for w in yev], wgt=wgt, alt=True)
        evs = []
        for c in range(DC):
            e1 = S.op("dve", lambda e, c=c: e.scalar_tensor_tensor(
                out=Y[:, c, c0:T], in0=Y[:, c, c0:T], scalar=gain(n, c), in1=RSTD[:, c0:T],
                op0=ALU.mult, op1=ALU.mult), [ev_rstd, yev[c]])
            eng = "dve" if c % 2 == 0 else "pool"
            evs.append(S.op(eng, lambda e, c=c: e.tensor_tensor(
                out=X[:, c, c0:T], in0=X[:, c, c0:T], in1=Y[:, c, c0:T], op=ALU.add), [e1]))
        return evs

    def ffn(n_pre, n_post, c0, xwaits):
        hev = prenorm(n_pre, c0, xwaits)
        yev = []
        sg_free = [None, None]
        actb_free = None
        sgi = 0
        for hf, (fb0, fb1) in enumerate(((0, NFB0), (NFB0, NFB))):
            act_evs = []
            for fb in range(fb0, fb1):
                banks = alloc_banks(4)
                bfree = sum((take_free(b) for b in banks), [])
                last = None
                for k in range(DC):
                    def mk(wb, k=k, banks=banks):
                        return [lambda e, j=j, k=k, banks=banks, wb=wb: e.matmul(
                            PS[:, banks[j], c0:T], wb[:, j * 128:(j + 1) * 128], H[:, k, c0:T],
                            start=(k == 0), stop=(k == DC - 1)) for j in range(4)]
                    last = wblock(mk, ([hev[k]] + (bfree if k == 0 else [])))
                for j in range(2):
                    buf = sgi % 2
                    sgi += 1
                    e_s = S.op("act", lambda e, j=j, buf=buf, banks=banks: e.activation(
                        out=SG[:, buf, c0:T], in_=PS[:, banks[j], c0:T], func=AF.Silu),
                        [last, sg_free[buf]])
                    fi = 2 * (fb - fb0) + j
                    e_m = S.op("dve", lambda e, j=j, buf=buf, banks=banks, fi=fi: e.tensor_tensor(
                        out=ACTB[:, fi, c0:T], in0=SG[:, buf, c0:T], in1=PS[:, banks[2 + j], c0:T],
                        op=ALU.mult), [e_s, last, actb_free])
                    bank_free[banks[j]] = [e_m]
                    bank_free[banks[2 + j]] = [e_m]
                    act_evs.append(e_m)
                    sg_free[buf] = e_m
            nfc = 2 * (fb1 - fb0)
            for dg in range(DG):
                banks = alloc_banks(4)
                bfree = sum((take_free(b) for b in banks), [])
                last = None
                for fc in range(nfc):
                    def mk(wb, fc=fc, banks=banks, nfc=nfc):
                        return [lambda e, j=j, fc=fc, banks=banks, wb=wb, nfc=nfc: e.matmul(
                            PS[:, banks[j], c0:T], wb[:, j * 128:(j + 1) * 128], ACTB[:, fc, c0:T],
                            start=(fc == 0), stop=(fc == nfc - 1)) for j in range(4)]
                    last = wblock(mk, ([act_evs[fc]] + (bfree if fc == 0 else [])))
                for j in range(4):
                    dt_ = dg * 4 + j
                    if hf == 0:
                        eng = "act" if j % 2 == 0 else "dve"
                        ev = S.op(eng, copy_fn(eng, Y[:, dt_, c0:T], PS[:, banks[j], c0:T]), [last])
                    else:
                        ev = S.op("dve", lambda e, dt_=dt_, b=banks[j]: e.tensor_tensor(
                            out=Y[:, dt_, c0:T], in0=PS[:, b, c0:T], in1=Y[:, dt_, c0:T], op=ALU.add),
                            [last, yev[dt_]])
                    bank_free[banks[j]] = [ev]
                    if hf == 0:
                        yev.append(ev)
                    else:
                        yev[dt_] = ev
            actb_free = last
        return postnorm(n_post, c0, 0.5, yev)

    rope_st = {"qf_free": [None] * 4, "t1_free": None, "n": 0}

    def rope_evac(bank, dst, cols, mm_ev, extra, done):
        a, b_ = cols
        i = rope_st["n"] % 4
        rope_st["n"] += 1
        e0 = S.op("act", copy_fn("act", QF[:, i, a:b_], PS[:, bank, a:b_]),
                  [mm_ev, rope_st["qf_free"][i]] + extra)
        bank_free[bank] = [e0]

        def rest():
            e1 = S.op("dve", lambda e: e.tensor_tensor(
                out=T1[0:64, a:b_], in0=QF[64:128, i, a:b_], in1=SIN[64:128, a:b_], op=ALU.mult),
                [e0, rope_st["t1_free"]] + extra)
            e2 = S.op("dve", lambda e: e.tensor_tensor(
                out=T1[64:128, a:b_], in0=QF[0:64, i, a:b_], in1=SIN[0:64, a:b_], op=ALU.mult),
                [e0, rope_st["t1_free"]] + extra)
            e3 = S.op("dve", lambda e: e.tensor_tensor(
                out=QF[:, i, a:b_], in0=QF[:, i, a:b_], in1=COS[:, a:b_], op=ALU.mult), [e0, e1, e2] + extra)
            e4 = S.op("dve", lambda e: e.tensor_tensor(
                out=dst, in0=T1[:, a:b_], in1=QF[:, i, a:b_], op=ALU.add), [e1, e2, e3])
            rope_st["qf_free"][i] = e4
            rope_st["t1_free"] = e4
            done(e4)
        return rest

    def mixer(ti, c0, xwaits, carry):
        hev = prenorm(2, 0, xwaits)
        ev_rope = S.dma(lambda e: e.dma_start(
            out=ROPE[:, :].rearrange("p (a b) -> p a b", b=T), in_=rope_d[:, :, ti * T:(ti + 1) * T]),
            "rope", carry.get("rope_free", []))
        ev_invc = None
        if ti == 0:
            ev_invc = S.dma(lambda e: e.dma_start(out=INVC, in_=invc_d), "invc", xwaits)
        hist = carry.get("hist", [])
        q_ev = [None] * NH
        k_ev = [None] * NKV
        v_ev = [None] * (T // 128)
        p_ev = [None] * PT
        st["nopool"] = True
        deferred = []

        def flush_one():
            if deferred:
                deferred.pop(0)()
        for (kind, _s, w, gi_) in cfg.cgs:
            if kind == "p":
                continue
            if kind == "v":
                nb_ = T // 128
                banks = alloc_banks(nb_)
                bfree = sum((take_free(b) for b in banks), [])
                last = None
                for k in range(DC):
                    def mk(wb, k=k, banks=banks, w=w):
                        return [lambda e, tb=tb, k=k, banks=banks, wb=wb, w=w: e.matmul(
                            PS[:, banks[tb], 0:w], H[:, k, tb * 128:(tb + 1) * 128], wb[:, 0:w],
                            start=(k == 0), stop=(k == DC - 1)) for tb in range(nb_)]
                    last = wblock(mk, ([hev[k]] + (bfree + carry.get("vt_free", []) if k == 0 else [])))
                    if k % 8 == 7:
                        flush_one()
                for tb in range(nb_):
                    eng = "act" if tb % 2 == 0 else "dve"
                    ev = S.op(eng, copy_fn(eng, VT[:, 1 + tb, :], PS[:, banks[tb], 0:w]),
                              [last] + carry.get("vt_free", []))
                    bank_free[banks[tb]] = [ev]
                    v_ev[tb] = ev
                continue
            nt = w // 128
            cc0 = c0 if kind == "q" else 0
            banks = alloc_banks(nt)
            bfree = sum((take_free(b) for b in banks), [])
            last = None
            for k in range(DC):
                def mk(wb, k=k, banks=banks, nt=nt, cc0=cc0):
                    return [lambda e, j=j, k=k, banks=banks, wb=wb, cc0=cc0: e.matmul(
                        PS[:, banks[j], cc0:T], wb[:, j * 128:(j + 1) * 128], H[:, k, cc0:T],
                        start=(k == 0), stop=(k == DC - 1)) for j in range(nt)]
                last = wblock(mk, ([hev[k]] + (bfree if k == 0 else [])))
                if k % 8 == 7:
                    flush_one()
            while deferred:
                flush_one()
            for j in range(nt):
                if kind == "q":
                    h = gi_ * 4 + j
                    deferred.append(rope_evac(banks[j], QR[:, h, cc0:T], (cc0, T), last, [ev_rope],
                                              lambda ev, h=h: q_ev.__setitem__(h, ev)))
                else:
                    deferred.append(rope_evac(banks[j], KR[:, j, 128 + cc0:128 + T], (cc0, T), last,
                                              [ev_rope] + carry.get("kr_free", []),
                                              lambda ev, j=j: k_ev.__setitem__(j, ev)))
        while deferred:
            flush_one()
        ev_ph = None
        if ti > 0:
            ev_ph = S.op("pool", lambda e: e.tensor_copy(out=P[:, :, 0:16], in_=PH), hist)
        pbanks = alloc_banks(4)
        abanks = [b for b in range(8) if b not in pbanks]
        arr = [0]

        def att_bank():
            b = abanks[arr[0] % len(abanks)]
            arr[0] += 1
            return b

        lo = 16 + c0
        n = T - c0
        pool_st = {"s_free": [None, None], "ph": []}

        def pool_tile(pt):
            g = pt // PGT
            w = 2 << g
            src = P[:, pt, :]
            e_ph = S.op("pool", lambda e: e.tensor_copy(out=PH[:, pt, :], in_=src[:, T:T + 16]), [p_ev[pt]])
            pool_st["ph"].append(e_ph)
            cur = None
            stp = 1
            bufs = [S1, S2]
            bi = 0
            evp = [p_ev[pt], ev_ph, ev_invc, e_ph]
            prev_ev = None
            while stp < w:
                ext = w - 2 * stp
                a = lo - ext
                dstb = bufs[bi]
                base = src if cur is None else cur
                in0 = base[:, a:lo + n]
                in1 = base[:, a - stp:lo + n - stp]
                prev_ev = S.op("pool", lambda e, dstb=dstb, in0=in0, in1=in1, a=a: e.tensor_tensor(
                    out=dstb[:, a:lo + n], in0=in0, in1=in1, op=ALU.add),
                    evp + [prev_ev, pool_st["s_free"][bi]])
                cur = dstb
                bi ^= 1
                stp *= 2
            if ti == 0:
                e1 = S.op("pool", lambda e: e.tensor_tensor(
                    out=cur[:, lo:lo + n], in0=cur[:, lo:lo + n], in1=INVC[:, g, 0:n], op=ALU.mult), [prev_ev])
            else:
                e1 = S.op("pool", lambda e: e.tensor_scalar(
                    out=cur[:, lo:lo + n], in0=cur[:, lo:lo + n], scalar1=1.0 / w, scalar2=None, op0=ALU.mult),
                    [prev_ev])
            ev_ = S.op("pool", lambda e: e.tensor_tensor(
                out=src[:, lo:lo + n], in0=cur[:, lo:lo + n], in1=src[:, lo:lo + n], op=ALU.subtract), [e1, e_ph])
            pool_st["s_free"] = [ev_, ev_]
            return ev_

        p_groups = [cg for cg in cfg.cgs if cg[0] == "p"]
        pst_ = {"g": 0, "k": 0, "last": None, "bfree": None}
        pool_ev = [None] * PT
        n_pblocks = len(p_groups) * DC

        def emit_pblocks(cnt):
            for _ in range(cnt):
                if pst_["g"] >= len(p_groups):
                    return
                gi_ = p_groups[pst_["g"]][3]
                k = pst_["k"]
                if k == 0:
                    pst_["bfree"] = sum((take_free(b) for b in pbanks), [])

                def mk(wb, k=k):
                    return [lambda e, j=j, k=k, wb=wb: e.matmul(
                        PS[:, pbanks[j], 0:T], wb[:, j * 128:(j + 1) * 128], H[:, k, 0:T],
                        start=(k == 0), stop=(k == DC - 1)) for j in range(4)]
                pst_["last"] = wblock(mk, ([hev[k]] + (pst_["bfree"] if k == 0 else [])))
                pst_["k"] += 1
                if pst_["k"] == DC:
                    for j in range(4):
                        pt = gi_ * 4 + j
                        ev = S.op("act", copy_fn("act", P[:, pt, 16:16 + T], PS[:, pbanks[j], 0:T]), [pst_["last"]])
                        bank_free[pbanks[j]] = [ev]
                        p_ev[pt] = ev
                    for j in range(4):
                        pt = gi_ * 4 + j
                        pool_ev[pt] = pool_tile(pt)
                    pst_["g"] += 1
                    pst_["k"] = 0

        scale = 128.0 ** -0.5
        ao_ev = []
        qbs = [qb for qb in range(T // 128) if qb * 128 >= c0]
        pairs = [(qb, h) for qb in qbs for h in range(NH)]
        set_free = [[] for _ in range(NSET)]
        pst = {}
        v_all = [v_ev[t_] for t_ in range(T // 128)]

        def att_a1(n_):
            qb, h = pairs[n_]
            g, i, qc = h // 4, n_ % NSET, qb * 128
            midx = 0 if (ti == 0 and qb == qbs[0]) else 1
            bsc = att_bank()
            e_sc = S.op("pe", lambda e: e.matmul(
                PS[:, bsc, 0:256], QR[:, h, qc:qc + 128], KR[:, g, qc:qc + 256], start=True, stop=True),
                [q_ev[h], k_ev[g]] + hist + take_free(bsc))
            e_s = S.op("dve", lambda e: e.scalar_tensor_tensor(
                out=SS[:, i, :], in0=PS[:, bsc, 0:256], scalar=scale, in1=MASK[:, midx, :],
                op0=ALU.mult, op1=ALU.add), [e_sc] + set_free[i] + const_evs)
            bank_free[bsc] = [e_s]
            e_mx = S.op("dve", lambda e: e.reduce_max(out=VEC[:, i, 0:1], in_=SS[:, i, :], axis=AX.X), [e_s])
            e_nm = S.op("dve", lambda e: e.tensor_scalar(
                out=VEC[:, i, 1:2], in0=VEC[:, i, 0:1], scalar1=SINKS[:, h:h + 1], scalar2=-1.0,
                op0=ALU.max, op1=ALU.mult), [e_mx])
            e_e = S.op("act", lambda e: e.activation(
                out=SS[:, i, :], in_=SS[:, i, :], func=AF.Exp, bias=VEC[:, i, 1:2], scale=1.0,
                accum_out=VEC[:, i, 2:3]), [e_nm])
            e_es = S.op("act", lambda e: e.activation(
                out=VEC[:, i, 3:4], in_=SINKS[:, h:h + 1], func=AF.Exp, bias=VEC[:, i, 1:2], scale=1.0),
                [e_nm, e_e])
            pst[n_] = dict(e_e=e_e, e_es=e_es)

        def att_a2(n_):
            i = n_ % NSET
            d = pst[n_]
            e_dn = S.op("dve", lambda e: e.tensor_tensor(
                out=VEC[:, i, 4:5], in0=VEC[:, i, 2:3], in1=VEC[:, i, 3:4], op=ALU.add), [d["e_e"], d["e_es"]])
            e_r = S.op("dve", lambda e: e.reciprocal(out=VEC[:, i, 5:6], in_=VEC[:, i, 4:5]), [e_dn])
            d["e_p"] = S.op("dve", lambda e: e.tensor_scalar(
                out=PB[:, i, :], in0=SS[:, i, :], scalar1=VEC[:, i, 5:6], scalar2=None, op0=ALU.mult), [e_r])

        def att_b(n_):
            i = n_ % NSET
            d = pst[n_]
            btr = att_bank()
            fr = take_free(btr)
            e_t = None
            for kb in range(2):
                e_t = S.op("pe", lambda e, kb=kb: e.transpose(
                    out=PSB[:, btr, kb * 128:(kb + 1) * 128], in_=PB[:, i, kb * 128:(kb + 1) * 128],
                    identity=IDENT[:, :]), [d["e_p"], ev_ident] + (fr if kb == 0 else []), signal=(kb == 1))
            d["e_pt"] = S.op("act", copy_fn("act", PTT[:, i, :], PSB[:, btr, 0:256]), [e_t])
            bank_free[btr] = [d["e_pt"]]

        def att_c(n_):
            qb, h = pairs[n_]
            g, i, qc = h // 4, n_ % NSET, qb * 128
            d = pst.pop(n_)
            bo = att_bank()
            fr = take_free(bo)
            e_o = None
            for kb in range(2):
                e_o = S.op("pe", lambda e, kb=kb: e.matmul(
                    PS[:, bo, 0:128], VT[:, qb + kb, g * 128:(g + 1) * 128], PTT[:, i, kb * 128:(kb + 1) * 128],
                    start=(kb == 0), stop=(kb == 1)),
                    [d["e_pt"]] + v_all + hist + (fr if kb == 0 else []), signal=(kb == 1))
            e_ao = S.op("dve", lambda e: e.tensor_copy(out=AO[:, h, qc:qc + 128], in_=PS[:, bo, 0:128]), [e_o])
            bank_free[bo] = [e_ao]
            set_free[i] = [e_ao, e_o]
            ao_ev.append(e_ao)

        npair = len(pairs)
        nsteps = npair + 3
        done_blocks = 0
        for step_ in range(nsteps):
            if step_ < npair:
                att_a1(step_)
            if 0 <= step_ - 1 < npair:
                att_a2(step_ - 1)
            if 0 <= step_ - 2 < npair:
                att_b(step_ - 2)
            if 0 <= step_ - 3 < npair:
                att_c(step_ - 3)
            tgt = (n_pblocks * (step_ + 1)) // nsteps
            emit_pblocks(tgt - done_blocks)
            done_blocks = tgt
        emit_pblocks(n_pblocks)
        last_in = pst_["last"]
        st["nopool"] = False
        ncarry = {}
        pd_ev = [None] * PT
        for pt in range(PT):
            eng = "act" if pt % 2 == 0 else "dve"
            pd_ev[pt] = S.op(eng, copy_fn(eng, PD[:, pt, c0:T], P[:, pt, lo:lo + n]), [pool_ev[pt], last_in])
        e_kc = S.op("pool", lambda e: e.tensor_copy(out=KR[:, :, 0:128], in_=KR[:, :, T:T + 128]),
                    [ev for ev in k_ev] + ao_ev)
        e_vc = S.op("pool", lambda e: e.tensor_copy(out=VT[:, 0, :], in_=VT[:, T // 128, :]),
                    [ev for ev in v_ev] + ao_ev)
        ncarry["hist"] = [e_kc, e_vc] + pool_st["ph"]
        ncarry["kr_free"] = [e_kc]
        ncarry["vt_free"] = [e_vc]
        pm_ev = [None] * PT
        for g in range(4):
            banks = alloc_banks(PGT)
            bfree = sum((take_free(b) for b in banks), [])
            last = None
            for c in range(PGT):
                def mk(wb, c=c, banks=banks, g=g):
                    return [lambda e, j=j, c=c, banks=banks, wb=wb, g=g: e.matmul(
                        PS[:, banks[j], c0:T], wb[:, j * 128:(j + 1) * 128], PD[:, g * PGT + c, c0:T],
                        start=(c == 0), stop=(c == PGT - 1)) for j in range(PGT)]
                last = wblock(mk, ([pd_ev[g * PGT + c]] + (bfree if c == 0 else [])))
            for j in range(PGT):
                pt = g * PGT + j
                ev = S.op("dve", lambda e, pt=pt, b=banks[j]: e.tensor_scalar(
                    out=PM[:, pt, c0:T], in0=PS[:, b, c0:T], scalar1=PSC[:, pt:pt + 1], scalar2=None,
                    op0=ALU.mult), [last] + pd_ev)
                bank_free[banks[j]] = [ev]
                pm_ev[pt] = ev
        if dbg == "mixdump":
            allev = ao_ev + pm_ev + pd_ev + [ev for ev in q_ev + k_ev + v_ev + p_ev if ev is not None]
            for (nm, ap_, shp, dt_) in (("d_qr", QR, [128, NH, T], BF16), ("d_kr", KR, [128, NKV, 128 + T], BF16),
                                        ("d_vt", VT, [128, 4, cfg.KVW], BF16), ("d_p", P, [128, PT, 16 + T], F32),
                                        ("d_pd", PD, [128, PT, T], BF16), ("d_pm", PM, [128, PT, T], BF16),
                                        ("d_ao", AO, [128, NH, T], BF16)):
                dd = nc.dram_tensor(nm, shp, dt_, kind="ExternalOutput").ap()
                S.dma(lambda e, dd=dd, ap_=ap_: e.dma_start(out=dd, in_=ap_), "dbgout", allev)
            return allev, ncarry
        yev = [None] * DC
        mix_ready = ao_ev + pm_ev
        for dg in range(DG):
            banks = alloc_banks(4)
            bfree = sum((take_free(b) for b in banks), [])
            last = None
            for c in range(DC):
                src_ = AO[:, c, c0:T] if c < NH else PM[:, c - NH, c0:T]
                def mk(wb, c=c, banks=banks, src_=src_):
                    return [lambda e, j=j, c=c, banks=banks, wb=wb, src_=src_: e.matmul(
                        PS[:, banks[j], c0:T], wb[:, j * 128:(j + 1) * 128], src_,
                        start=(c == 0), stop=(c == DC - 1)) for j in range(4)]
                last = wblock(mk, ((mix_ready if (dg == 0 and c == 0) else []) + (bfree if c == 0 else [])))
            for j in range(4):
                dt_ = dg * 4 + j
                eng = "act" if j % 2 == 0 else "dve"
                ev = S.op(eng, copy_fn(eng, Y[:, dt_, c0:T], PS[:, banks[j], c0:T]),
                          [last] + pd_ev + ao_ev[-1:])
                bank_free[banks[j]] = [ev]
                yev[dt_] = ev
        ncarry["rope_free"] = [ao_ev[-1]] if ao_ev else []
        xev = postnorm(3, c0, 1.0, yev)
        return xev, ncarry

    carry = {}
    NX = 8 if DC % 8 == 0 else 1
    step = DC // NX
    out_ev = [None] * NX
    for ti in range(cfg.NT):
        c0 = cfg.HALO if ti == 0 else 0
        xld = []
        for q in range(NX):
            xld.append(S.dma(lambda e, q=q, ti=ti: e.dma_start(
                out=X[:, q * step:(q + 1) * step, :], in_=xT[:, q * step:(q + 1) * step, ti * T:(ti + 1) * T]),
                "xin%d" % q, [out_ev[q]]))
        xw = [xld[c // step] for c in range(DC)]
        xev = ffn(0, 1, 0, xw)
        if dbg == "ffn1":
            break
        xev, carry = mixer(ti, c0, xev, carry)
        if dbg in ("mix", "mixdump"):
            break
        xev = ffn(4, 5, c0, xev)
        if dbg == "ffn2":
            break
        o0 = ti * T - cfg.HALO + c0
        for q in range(NX):
            out_ev[q] = S.dma(lambda e, q=q, o0=o0, c0=c0: e.dma_start(
                out=outT[:, q * step:(q + 1) * step, o0:o0 + T - c0], in_=X[:, q * step:(q + 1) * step, c0:T]),
                "xout%d" % q, xev[q * step:(q + 1) * step])
    if dbg:
        dbg_d = nc.dram_tensor("dbg", [128, DC, T], F32, kind="ExternalOutput").ap()
        out_ev = [S.dma(lambda e: e.dma_start(out=dbg_d, in_=X), "dbgout", xev)]
    S.wait_only("sync", out_ev)
    assert dbg or st["pe"] == TOTAL, (st, TOTAL)

    esem = {}
    for e_ in ("act", "dve", "pool", "pe"):
        esem[("eng", e_)] = es.enter_context(nc.semaphore("s_" + e_))
    for name in S.dma_cnt:
        esem[("dma", name)] = es.enter_context(nc.semaphore("d_" + name))

    def replay(engname, eng):
        waited = {}
        for (fn, waits, signal, dmasem) in S.ops[engname]:
            for w in waits:
                if w.eng == engname and engname == "pe":
                    continue
                if waited.get(w.sem, 0) >= w.val:
                    continue
                eng.wait_ge(esem[w.sem], w.val)
                waited[w.sem] = w.val
            if fn is None:
                continue
            inst = fn(eng)
            if dmasem is not None:
                inst.then_inc(esem[("dma", dmasem)], 16)
            elif signal:
                inst.then_inc(esem[("eng", engname)], 1)

    with nc.Block() as block:
        @block.sync
        def _(e):
            replay("sync", e)

        @block.scalar
        def _(e):
            replay("act", e)

        @block.vector
        def _(e):
            replay("dve", e)

        @block.gpsimd
        def _(e):
            replay("pool", e)

        @block.tensor
        def _(e):
            replay("pe", e)
    es.close()
    return nc


_CACHE = {}


def kernel(**inputs):
    cfg = Cfg()
    in_maps = build_core_inputs(cfg, inputs)
    if "nc" not in _CACHE:
        _CACHE["nc"] = build_program(cfg)
    res = run_bass_kernel_spmd(_CACHE["nc"], in_maps, core_ids=list(range(cfg.NCORE)))
    return assemble_output(cfg, res.results)
```

```python
import contextlib
import numpy as np
import concourse.bass as bass
import concourse.mybir as mybir
from concourse.bass_utils import run_bass_kernel_spmd

F32 = mybir.dt.float32
BF16 = mybir.dt.bfloat16
AF = mybir.ActivationFunctionType
ALU = mybir.AluOpType
AX = mybir.AxisListType

RMS_EPS = 1e-6
ROPE_THETA = 10000.0
NEG = -30000.0


class Cfg:
    def __init__(self, D=4096, SEQ=2048, BATCH=4, NCORE=8, T=384, NS=11, NB=4,
                 cast_pattern=("act", "dve", "act", "dve", "pool", "act", "dve", "dve")):
        self.D = D
        self.DC = D // 128
        self.FF = ((8 * D // 3 + 255) // 256) * 256
        self.NFB = self.FF // 256
        self.NFB0 = (self.NFB + 1) // 2
        self.NH = D // 256
        self.NKV = self.NH // 4
        self.AW = self.NH * 128
        self.KVW = self.NKV * 128
        self.PW = D - self.AW
        self.PGW = self.PW // 4
        self.PT = self.PW // 128
        self.PGT = self.PGW // 128
        self.INW = self.AW + 2 * self.KVW + self.PW
        self.DG = D // 512
        self.SEQ, self.BATCH, self.NCORE = SEQ, BATCH, NCORE
        self.CPS = NCORE // BATCH
        self.TOK = SEQ // self.CPS
        self.HALO = 128
        self.T = T
        self.TOKH = self.TOK + self.HALO
        self.NT = self.TOKH // T
        assert self.NT * T == self.TOKH and T % 128 == 0
        self.NS, self.NB = NS, NB
        self.cast_pattern = cast_pattern
        cgs = []
        nq, npg = self.AW // 512, self.PW // 512
        for i in range(nq):
            cgs.append(("q", i * 512, 512, i))
        cgs.append(("k", self.AW, self.KVW, 0))
        cgs.append(("v", self.AW + self.KVW, self.KVW, 0))
        for i in reversed(range(npg)):
            cgs.append(("p", self.AW + 2 * self.KVW + i * 512, 512, i))
        self.cgs = cgs
        self.NBLK = (2 * (self.NFB * self.DC + self.DG * 2 * self.NFB)
                     + len(cgs) * self.DC + 4 * self.PGT + self.DG * self.DC)


def build_wstream(cfg, W):
    DC, NFB, NFB0, DG = cfg.DC, cfg.NFB, cfg.NFB0, cfg.DG
    out = np.zeros((cfg.NBLK, 128, 512), np.float32)
    pos = 0

    def ffn(wg, wu, wd):
        nonlocal pos
        g = wg.reshape(DC, 128, NFB, 256).transpose(2, 0, 1, 3)
        u = wu.reshape(DC, 128, NFB, 256).transpose(2, 0, 1, 3)
        d = wd.reshape(2 * NFB, 128, DG, 512).transpose(2, 0, 1, 3)
        for (fb0, fb1) in ((0, NFB0), (NFB0, NFB)):
            n = (fb1 - fb0) * DC
            blk = out[pos:pos + n].reshape(fb1 - fb0, DC, 128, 512)
            blk[..., 0:256] = g[fb0:fb1]
            blk[..., 256:512] = u[fb0:fb1]
            pos += n
            nfc = 2 * (fb1 - fb0)
            n = DG * nfc
            out[pos:pos + n] = d[:, 2 * fb0:2 * fb1].reshape(n, 128, 512)
            pos += n

    ffn(W["ffn1_w_gate"], W["ffn1_w_up"], W["ffn1_w_down"])
    win = W["w_in"]
    for (_kind, s, w, _i) in cfg.cgs:
        out[pos:pos + DC, :, 0:w] = win[:, s:s + w].reshape(DC, 128, w)
        pos += DC
    pw = W["pool_w"]
    for g in range(4):
        out[pos:pos + cfg.PGT, :, 0:cfg.PGW] = pw[g].reshape(cfg.PGT, 128, cfg.PGW)
        pos += cfg.PGT
    wo = W["w_out"].reshape(DC, 128, DG, 512).transpose(2, 0, 1, 3)
    out[pos:pos + DG * DC] = wo.reshape(DG * DC, 128, 512)
    pos += DG * DC
    ffn(W["ffn2_w_gate"], W["ffn2_w_up"], W["ffn2_w_down"])
    assert pos == cfg.NBLK, (pos, cfg.NBLK)
    return out


def build_core_inputs(cfg, inputs):
    f = np.float32
    DC = cfg.DC
    W = {k: np.asarray(v, f)[0] for k, v in inputs.items() if k != "x"}
    x = np.asarray(inputs["x"], f)
    wst = build_wstream(cfg, W)
    gnames = ["ffn1_pre_g", "ffn1_post_g", "mix_pre_g", "mix_post_g", "ffn2_pre_g", "ffn2_post_g"]
    gains = np.stack([W[n].reshape(DC, 128).T for n in gnames], axis=1)
    gains = np.ascontiguousarray(gains, f)
    sinks = np.ascontiguousarray(np.broadcast_to(W["attn_sinks"][None, :], (128, cfg.NH)), f)
    psc = np.ascontiguousarray(W["pool_scale"].reshape(cfg.PT, 128).T, f)
    ident = np.eye(128, dtype=f)
    inv_freq = (f(ROPE_THETA) ** (-np.arange(0, 128, 2, dtype=f) / f(128))).astype(f)
    qi = np.arange(128)[:, None]
    ki = np.arange(256)[None, :]
    diff = qi + 128 - ki
    band = (diff >= 0) & (diff < 128)
    in_maps = []
    for core in range(cfg.NCORE):
        b, s = divmod(core, cfg.CPS)
        s0 = s * cfg.TOK
        xT = np.zeros((128, DC, cfg.TOKH), f)
        lo = s0 - cfg.HALO
        src_lo = max(lo, 0)
        seg = x[b, src_lo:s0 + cfg.TOK]
        xT[:, :, src_lo - lo:] = seg.T.reshape(DC, 128, -1).transpose(1, 0, 2)
        pos = (np.arange(cfg.TOKH) + lo).astype(f)
        ang = (pos[:, None] * inv_freq[None, :]).astype(f)
        c = np.cos(ang).astype(f).T
        sn = np.sin(ang).astype(f).T
        rope = np.stack([np.concatenate([c, c], 0), np.concatenate([sn, -sn], 0)], axis=1)
        rope = np.ascontiguousarray(rope, f)
        m0 = band.copy()
        if s0 == 0:
            m0 &= (ki >= 128)
        masks = np.stack([np.where(m0, 0.0, NEG), np.where(band, 0.0, NEG)], axis=1).astype(f)
        t = np.arange(256)
        invc = np.zeros((128, 4, 256), f)
        for g, w in enumerate((2, 4, 8, 16)):
            cnt = np.minimum(t + 1, w) if s0 == 0 else np.full(256, w)
            invc[:, g, :] = (1.0 / cnt.astype(f)).astype(f)[None, :]
        in_maps.append({"xT": xT, "wst": wst, "gains": gains, "sinks": sinks, "psc": psc,
                        "ident": ident, "rope": rope, "masks": masks, "invc": invc})
    return in_maps


def assemble_output(cfg, results):
    out = np.zeros((cfg.BATCH, cfg.SEQ, cfg.D), np.float32)
    for core in range(cfg.NCORE):
        b, s = divmod(core, cfg.CPS)
        oT = np.asarray(results[core]["outT"])
        out[b, s * cfg.TOK:(s + 1) * cfg.TOK] = oT.transpose(2, 1, 0).reshape(cfg.TOK, cfg.D)
    return out


class Ev:
    __slots__ = ("sem", "val", "eng")

    def __init__(self, sem, val, eng):
        self.sem, self.val, self.eng = sem, val, eng


class Sched:
    ENG = ("sync", "act", "dve", "pool", "pe")

    def __init__(self):
        self.ops = {e: [] for e in self.ENG}
        self.cnt = {e: 0 for e in self.ENG}
        self.dma_cnt = {}

    def op(self, eng, fn, waits=(), signal=True):
        ws = [w for w in waits if w is not None]
        ev = None
        if signal:
            self.cnt[eng] += 1
            ev = Ev(("eng", eng), self.cnt[eng], eng)
        self.ops[eng].append((fn, ws, signal, None))
        return ev

    def dma(self, fn, semname, waits=()):
        ws = [w for w in waits if w is not None]
        self.dma_cnt[semname] = self.dma_cnt.get(semname, 0) + 16
        ev = Ev(("dma", semname), self.dma_cnt[semname], "dma")
        self.ops["sync"].append((fn, ws, False, semname))
        return ev

    def wait_only(self, eng, waits):
        self.ops[eng].append((None, [w for w in waits if w is not None], False, None))


def build_program(cfg, dbg=None):
    DC, T, NS, NB = cfg.DC, cfg.T, cfg.NS, cfg.NB
    NH, NKV, PT, PGT, DG = cfg.NH, cfg.NKV, cfg.PT, cfg.PGT, cfg.DG
    NFB, NFB0 = cfg.NFB, cfg.NFB0
    NBLK = cfg.NBLK
    D = cfg.D
    nc = bass.Bass("TRN2", target_bir_lowering=False)
    xT = nc.dram_tensor("xT", [128, DC, cfg.TOKH], F32, kind="ExternalInput").ap()
    wst = nc.dram_tensor("wst", [NBLK, 128, 512], F32, kind="ExternalInput").ap()
    gains_d = nc.dram_tensor("gains", [128, 6, DC], F32, kind="ExternalInput").ap()
    sinks_d = nc.dram_tensor("sinks", [128, NH], F32, kind="ExternalInput").ap()
    psc_d = nc.dram_tensor("psc", [128, PT], F32, kind="ExternalInput").ap()
    ident_d = nc.dram_tensor("ident", [128, 128], F32, kind="ExternalInput").ap()
    rope_d = nc.dram_tensor("rope", [128, 2, cfg.TOKH], F32, kind="ExternalInput").ap()
    masks_d = nc.dram_tensor("masks", [128, 2, 256], F32, kind="ExternalInput").ap()
    invc_d = nc.dram_tensor("invc", [128, 4, 256], F32, kind="ExternalInput").ap()
    outT = nc.dram_tensor("outT", [128, DC, cfg.TOK], F32, kind="ExternalOutput").ap()

    S = Sched()
    es = contextlib.ExitStack()

    def sb(name, n, dt):
        return es.enter_context(nc.sbuf_tensor(name, [128, n], dt))

    def v3(t, off, a, b):
        return t[:, off:off + a * b].rearrange("p (a b) -> p a b", b=b)

    RA = sb("RA", DC * T, F32)
    RB = sb("RB", DC * T, F32)
    RC = sb("RC", DC * T, BF16)
    nd = max(2 * NFB0, 2 * NH) * T
    RD = sb("RD", nd, BF16)
    KRt = sb("KR", NKV * (128 + T), BF16)
    VTt = sb("VT", 4 * cfg.KVW, BF16)
    PHt = sb("PH", PT * 16, F32)
    STG = sb("STG", NS * 512, F32)
    WBt = sb("WB", NB * 512, BF16)
    SGt = sb("SG", 2 * T, F32)
    NRM = sb("NRM", 4 * T, F32)
    GAINS = sb("GAINS", 6 * DC, F32)
    SINKS = sb("SINKS", NH, F32)
    PSC = sb("PSC", PT, F32)
    IDENT = sb("IDENT", 128, BF16)
    ONES = sb("ONES", 128, F32)
    MASKS = sb("MASKS", 512, F32)
    ROPE = sb("ROPE", 2 * T, F32)

    X = v3(RA, 0, DC, T)
    Y = v3(RB, 0, DC, T)
    P = v3(RB, 0, PT, 16 + T)
    H = v3(RC, 0, DC, T)
    PD = v3(RC, 0, PT, T)
    PM = v3(RC, PT * T, PT, T)
    ACTB = v3(RD, 0, 2 * NFB0, T)
    QR = v3(RD, 0, NH, T)
    AO = v3(RD, NH * T, NH, T)
    KR = v3(KRt, 0, NKV, 128 + T)
    VT = v3(VTt, 0, 4, cfg.KVW)
    PH = v3(PHt, 0, PT, 16)
    SG = v3(SGt, 0, 2, T)
    SQ = v3(NRM, 0, 2, T)
    RS = NRM[:, 2 * T:3 * T]
    RSTD = NRM[:, 3 * T:4 * T]
    COS = ROPE[:, 0:T]
    SIN = ROPE[:, T:2 * T]
    MASK = v3(MASKS, 0, 2, 256)

    NSET = 4
    need = 5 * T + NSET * (256 + 128 + 128 + 16) + 2 * (T + 16) + 1024
    p_end = PT * (16 + T)
    if DC * T - p_end >= need:
        SCR, so = RB, p_end
    else:
        SCR, so = sb("SCR", need, F32), 0
    SCRB = SCR.bitcast(BF16)

    def carve(n):
        nonlocal so
        o = so
        so += n
        return o

    o = carve(4 * T); QF = v3(SCR, o, 4, T)
    o = carve(T); T1 = SCR[:, o:o + T]
    o = carve(NSET * 256); SS = v3(SCR, o, NSET, 256)
    o = carve(NSET * 128); PB = v3(SCRB, 2 * o, NSET, 256)
    o = carve(NSET * 128); PTT = v3(SCRB, 2 * o, NSET, 256)
    o = carve(NSET * 16); VEC = v3(SCR, o, NSET, 16)
    o = carve(T + 16); S1 = SCR[:, o:o + T + 16]
    o = carve(T + 16); S2 = SCR[:, o:o + T + 16]
    o = carve(1024); INVC = v3(SCR, o, 4, 256)

    PS = es.enter_context(nc.psum_tensor("PS", [128, 8, 512], F32))
    PSB = PS.bitcast(BF16)

    TOTAL = NBLK * cfg.NT
    st = {"dma": 0, "cast": 0, "pe": 0}
    dma_ev, cast_ev, pe_ev = {}, {}, {}
    cast_engs = cfg.cast_pattern

    def copy_fn(eng, out, in_):
        if eng == "act":
            return lambda e: e.activation(out=out, in_=in_, func=AF.Copy)
        return lambda e: e.tensor_copy(out=out, in_=in_)

    def pump():
        while True:
            prog = False
            j = st["dma"]
            if j < TOTAL and j - NS < st["cast"]:
                slot = j % NS
                dst = STG[:, slot * 512:(slot + 1) * 512]
                src = wst[j % NBLK]
                dma_ev[j] = S.dma(lambda e, d=dst, s_=src: e.dma_start(out=d, in_=s_),
                                  "stg%d" % slot, [cast_ev.get(j - NS)])
                st["dma"] += 1
                prog = True
            j = st["cast"]
            if j < st["dma"] and j - NB < st["pe"]:
                eng = cast_engs[j % len(cast_engs)]
                if eng == "pool" and st.get("nopool"):
                    eng = "act" if (j // len(cast_engs)) % 2 == 0 else "dve"
                src = STG[:, (j % NS) * 512:(j % NS + 1) * 512]
                dst = WBt[:, (j % NB) * 512:(j % NB + 1) * 512]
                cast_ev[j] = S.op(eng, copy_fn(eng, dst, src), [dma_ev[j], pe_ev.get(j - NB)])
                st["cast"] += 1
                prog = True
            if not prog:
                break

    def wblock(mk_ops, extra_waits=()):
        pump()
        j = st["pe"]
        assert j < st["cast"], "weight pipeline underflow"
        wb = WBt[:, (j % NB) * 512:(j % NB + 1) * 512]
        fns = mk_ops(wb)
        ev = None
        for i, fn in enumerate(fns):
            w = ([cast_ev[j]] + list(extra_waits)) if i == 0 else ()
            ev = S.op("pe", fn, w, signal=(i == len(fns) - 1))
        pe_ev[j] = ev
        st["pe"] += 1
        for d_ in (cast_ev, dma_ev, pe_ev):
            d_.pop(j - 4 * (NS + NB), None)
        return ev

    bank_free = [[] for _ in range(8)]
    bank_rr = [0]

    def alloc_banks(n):
        bs = []
        for _ in range(n):
            bs.append(bank_rr[0])
            bank_rr[0] = (bank_rr[0] + 1) % 8
        return bs

    def take_free(b):
        ev = bank_free[b]
        bank_free[b] = []
        return ev

    cev = []
    cev.append(S.dma(lambda e: e.dma_start(out=GAINS[:, :].rearrange("p (a b) -> p a b", b=DC), in_=gains_d), "cst"))
    cev.append(S.dma(lambda e: e.dma_start(out=SINKS[:, :], in_=sinks_d), "cst"))
    cev.append(S.dma(lambda e: e.dma_start(out=PSC[:, :], in_=psc_d), "cst"))
    cev.append(S.dma(lambda e: e.dma_start(out=MASK, in_=masks_d), "cst"))
    cev.append(S.dma(lambda e: e.dma_start(out=SGt[:, 0:128], in_=ident_d), "cst"))
    c_all = cev[-1]
    ev_ident = S.op("dve", lambda e: e.tensor_copy(out=IDENT[:, :], in_=SGt[:, 0:128]), [c_all])
    ev_ones = S.op("pool", lambda e: e.memset(ONES[:, :], 1.0), [])
    const_evs = [c_all, ev_ident, ev_ones]

    def gain(n, c):
        return GAINS[:, n * DC + c:n * DC + c + 1]

    state = {"x_ready": [], "sq_free": [None, None], "misc": []}

    def stats(src, c0, waits, chunk_waits=None, wgt=1.0, alt=False):
        b = alloc_banks(1)[0]
        bfree = take_free(b)
        last = None
        for c in range(DC):
            i = c % 2
            cw = list(chunk_waits[c]) if chunk_waits is not None else []
            if alt and c % 2 == 1:
                ev_sq = S.op("dve", lambda e, c=c, i=i: e.tensor_tensor(
                    out=SQ[:, i, c0:T], in0=src[:, c, c0:T], in1=src[:, c, c0:T], op=ALU.mult),
                    list(waits) + cw + [state["sq_free"][i]])
            else:
                ev_sq = S.op("act", lambda e, c=c, i=i: e.activation(
                    out=SQ[:, i, c0:T], in_=src[:, c, c0:T], func=AF.Square),
                    list(waits) + cw + [state["sq_free"][i]])
            last = S.op("pe", lambda e, c=c, i=i, b=b: e.matmul(
                PS[:, b, c0:T], ONES[:, :], SQ[:, i, c0:T], start=(c == 0), stop=(c == DC - 1)),
                [ev_sq, ev_ones] + (bfree if c == 0 else []))
            state["sq_free"][i] = last
        ev_rs = S.op("act", lambda e, b=b: e.activation(
            out=RS[:, c0:T], in_=PS[:, b, c0:T], func=AF.Sqrt, bias=EPSB[:, 0:1], scale=1.0 / D), [last])
        bank_free[b] = [ev_rs]
        ev_rstd = S.op("dve", lambda e: e.reciprocal(out=RSTD[:, c0:T], in_=RS[:, c0:T]), [ev_rs])
        if wgt != 1.0:
            ev_rstd = S.op("dve", lambda e: e.tensor_scalar(
                out=RSTD[:, c0:T], in0=RSTD[:, c0:T], scalar1=float(wgt), scalar2=None, op0=ALU.mult), [ev_rstd])
        return ev_rstd

    EPSB = sb("EPSB", 1, F32)
    ev_eps = S.op("pool", lambda e: e.memset(EPSB[:, :], RMS_EPS), [])
    const_evs.append(ev_eps)

    def prenorm(n, c0, waits):
        waits = list(waits)
        if len(waits) == DC:
            cw = [[w] for w in waits]
            ev_rstd = stats(X, c0, const_evs, cw)
        else:
            cw = [waits] * DC
            ev_rstd = stats(X, c0, waits + const_evs)
        evs = []
        for c in range(DC):
            evs.append(S.op("dve", lambda e, c=c: e.scalar_tensor_tensor(
                out=H[:, c, c0:T], in0=X[:, c, c0:T], scalar=gain(n, c), in1=RSTD[:, c0:T],
                op0=ALU.mult, op1=ALU.mult), [ev_rstd] + cw[c]))
        return evs

    def postnorm(n, c0, wgt, yev):
        ev_rstd = stats(Y, c0, [], [[w] for w in yev], wgt=wgt, alt=True)
        evs = []
        for c in range(DC):
            e1 = S.op("dve", lambda e, c=c: e.scalar_tensor_tensor(
                out=Y[:, c, c0:T], in0=Y[:, c, c0:T], scalar=gain(n, c), in1=RSTD[:, c0:T],
                op0=ALU.mult, op1=ALU.mult), [ev_rstd, yev[c]])
            eng = "dve" if c % 2 == 0 else "pool"
            evs.append(S.op(eng, lambda e, c=c: e.tensor_tensor(
                out=X[:, c, c0:T], in0=X[:, c, c0:T], in1=Y[:, c, c0:T], op=ALU.add), [e1]))
        return evs

    def ffn(n_pre, n_post, c0, xwaits):
        hev = prenorm(n_pre, c0, xwaits)
        yev = []
        sg_free = [None, None]
        actb_free = None
        sgi = 0
        for hf, (fb0, fb1) in enumerate(((0, NFB0), (NFB0, NFB))):
            act_evs = []
            for fb in range(fb0, fb1):
                banks = alloc_banks(4)
                bfree = sum((take_free(b) for b in banks), [])
                last = None
                for k in range(DC):
                    def mk(wb, k=k, banks=banks):
                        return [lambda e, j=j, k=k, banks=banks, wb=wb: e.matmul(
                            PS[:, banks[j], c0:T], wb[:, j * 128:(j + 1) * 128], H[:, k, c0:T],
                            start=(k == 0), stop=(k == DC - 1)) for j in range(4)]
                    last = wblock(mk, ([hev[k]] + (bfree if k == 0 else [])))
                for j in range(2):
                    buf = sgi % 2
                    sgi += 1
                    e_s = S.op("act", lambda e, j=j, buf=buf, banks=banks: e.activation(
                        out=SG[:, buf, c0:T], in_=PS[:, banks[j], c0:T], func=AF.Silu),
                        [last, sg_free[buf]])
                    fi = 2 * (fb - fb0) + j
                    e_m = S.op("dve", lambda e, j=j, buf=buf, banks=banks, fi=fi: e.tensor_tensor(
                        out=ACTB[:, fi, c0:T], in0=SG[:, buf, c0:T], in1=PS[:, banks[2 + j], c0:T],
                        op=ALU.mult), [e_s, last, actb_free])
                    bank_free[banks[j]] = [e_m]
                    bank_free[banks[2 + j]] = [e_m]
                    act_evs.append(e_m)
                    sg_free[buf] = e_m
            nfc = 2 * (fb1 - fb0)
            for dg in range(DG):
                banks = alloc_banks(4)
                bfree = sum((take_free(b) for b in banks), [])
                last = None
                for fc in range(nfc):
                    def mk(wb, fc=fc, banks=banks, nfc=nfc):
                        return [lambda e, j=j, fc=fc, banks=banks, wb=wb, nfc=nfc: e.matmul(
                            PS[:, banks[j], c0:T], wb[:, j * 128:(j + 1) * 128], ACTB[:, fc, c0:T],
                            start=(fc == 0), stop=(fc == nfc - 1)) for j in range(4)]
                    last = wblock(mk, ([act_evs[fc]] + (bfree if fc == 0 else [])))
                for j in range(4):
                    dt_ = dg * 4 + j
                    if hf == 0:
                        eng = "act" if j % 2 == 0 else "dve"
                        ev = S.op(eng, copy_fn(eng, Y[:, dt_, c0:T], PS[:, banks[j], c0:T]), [last])
                    else:
                        ev = S.op("dve", lambda e, dt_=dt_, b=banks[j]: e.tensor_tensor(
                            out=Y[:, dt_, c0:T], in0=PS[:, b, c0:T], in1=Y[:, dt_, c0:T], op=ALU.add),
                            [last, yev[dt_]])
                    bank_free[banks[j]] = [ev]
                    if hf == 0:
                        yev.append(ev)
                    else:
                        yev[dt_] = ev
            actb_free = last
        return postnorm(n_post, c0, 0.5, yev)

    rope_st = {"qf_free": [None] * 4, "t1_free": None, "n": 0}

    def rope_evac(bank, dst, cols, mm_ev, extra, done):
        a, b_ = cols
        i = rope_st["n"] % 4
        rope_st["n"] += 1
        e0 = S.op("act", copy_fn("act", QF[:, i, a:b_], PS[:, bank, a:b_]),
                  [mm_ev, rope_st["qf_free"][i]] + extra)
        bank_free[bank] = [e0]

        def rest():
            e1 = S.op("dve", lambda e: e.tensor_tensor(
                out=T1[0:64, a:b_], in0=QF[64:128, i, a:b_], in1=SIN[64:128, a:b_], op=ALU.mult),
                [e0, rope_st["t1_free"]] + extra)
            e2 = S.op("dve", lambda e: e.tensor_tensor(
                out=T1[64:128, a:b_], in0=QF[0:64, i, a:b_], in1=SIN[0:64, a:b_], op=ALU.mult),
                [e0, rope_st["t1_free"]] + extra)
            e3 = S.op("dve", lambda e: e.tensor_tensor(
                out=QF[:, i, a:b_], in0=QF[:, i, a:b_], in1=COS[:, a:b_], op=ALU.mult), [e0, e1, e2] + extra)
            e4 = S.op("dve", lambda e: e.tensor_tensor(
                out=dst, in0=T1[:, a:b_], in1=QF[:, i, a:b_], op=ALU.add), [e1, e2, e3])
            rope_st["qf_free"][i] = e4
            rope_st["t1_free"] = e4
            done(e4)
        return rest

    def mixer(ti, c0, xwaits, carry):
        hev = prenorm(2, 0, xwaits)
        ev_rope = S.dma(lambda e: e.dma_start(
            out=ROPE[:, :].rearrange("p (a b) -> p a b", b=T), in_=rope_d[:, :, ti * T:(ti + 1) * T]),
            "rope", carry.get("rope_free", []))
        ev_invc = None
        if ti == 0:
            ev_invc = S.dma(lambda e: e.dma_start(out=INVC, in_=invc_d), "invc", xwaits)
        hist = carry.get("hist", [])
        q_ev = [None] * NH
        k_ev = [None] * NKV
        v_ev = [None] * (T // 128)
        p_ev = [None] * PT
        st["nopool"] = True
        deferred = []

        def flush_one():
            if deferred:
                deferred.pop(0)()
        for (kind, _s, w, gi_) in cfg.cgs:
            if kind == "p":
                continue
            if kind == "v":
                nb_ = T // 128
                banks = alloc_banks(nb_)
                bfree = sum((take_free(b) for b in banks), [])
                last = None
                for k in range(DC):
                    def mk(wb, k=k, banks=banks, w=w):
                        return [lambda e, tb=tb, k=k, banks=banks, wb=wb, w=w: e.matmul(
                            PS[:, banks[tb], 0:w], H[:, k, tb * 128:(tb + 1) * 128], wb[:, 0:w],
                            start=(k == 0), stop=(k == DC - 1)) for tb in range(nb_)]
                    last = wblock(mk, ([hev[k]] + (bfree + carry.get("vt_free", []) if k == 0 else [])))
                    if k % 8 == 7:
                        flush_one()
                for tb in range(nb_):
                    eng = "act" if tb % 2 == 0 else "dve"
                    ev = S.op(eng, copy_fn(eng, VT[:, 1 + tb, :], PS[:, banks[tb], 0:w]),
                              [last] + carry.get("vt_free", []))
                    bank_free[banks[tb]] = [ev]
                    v_ev[tb] = ev
                continue
            nt = w // 128
            cc0 = c0 if kind == "q" else 0
            banks = alloc_banks(nt)
            bfree = sum((take_free(b) for b in banks), [])
            last = None
            for k in range(DC):
                def mk(wb, k=k, banks=banks, nt=nt, cc0=cc0):
                    return [lambda e, j=j, k=k, banks=banks, wb=wb, cc0=cc0: e.matmul(
                        PS[:, banks[j], cc0:T], wb[:, j * 128:(j + 1) * 128], H[:, k, cc0:T],
                        start=(k == 0), stop=(k == DC - 1)) for j in range(nt)]
                last = wblock(mk, ([hev[k]] + (bfree if k == 0 else [])))
                if k % 8 == 7:
                    flush_one()
            while deferred:
                flush_one()
            for j in range(nt):
                if kind == "q":
                    h = gi_ * 4 + j
                    deferred.append(rope_evac(banks[j], QR[:, h, cc0:T], (cc0, T), last, [ev_rope],
                                              lambda ev, h=h: q_ev.__setitem__(h, ev)))
                else:
                    deferred.append(rope_evac(banks[j], KR[:, j, 128 + cc0:128 + T], (cc0, T), last,
                                              [ev_rope] + carry.get("kr_free", []),
                                              lambda ev, j=j: k_ev.__setitem__(j, ev)))
        while deferred:
            flush_one()
        ev_ph = None
        if ti > 0:
            ev_ph = S.op("pool", lambda e: e.tensor_copy(out=P[:, :, 0:16], in_=PH), hist)
        pbanks = alloc_banks(4)
        abanks = [b for b in range(8) if b not in pbanks]
        arr = [0]

        def att_bank():
            b = abanks[arr[0] % len(abanks)]
            arr[0] += 1
            return b

        lo = 16 + c0
        n = T - c0
        pool_st = {"s_free": [None, None], "ph": []}

        def pool_tile(pt):
            g = pt // PGT
            w = 2 << g
            src = P[:, pt, :]
            e_ph = S.op("pool", lambda e: e.tensor_copy(out=PH[:, pt, :], in_=src[:, T:T + 16]), [p_ev[pt]])
            pool_st["ph"].append(e_ph)
            cur = None
            stp = 1
            bufs = [S1, S2]
            bi = 0
            evp = [p_ev[pt], ev_ph, ev_invc, e_ph]
            prev_ev = None
            while stp < w:
                ext = w - 2 * stp
                a = lo - ext
                dstb = bufs[bi]
                base = src if cur is None else cur
                in0 = base[:, a:lo + n]
                in1 = base[:, a - stp:lo + n - stp]
                prev_ev = S.op("pool", lambda e, dstb=dstb, in0=in0, in1=in1, a=a: e.tensor_tensor(
                    out=dstb[:, a:lo + n], in0=in0, in1=in1, op=ALU.add),
                    evp + [prev_ev, pool_st["s_free"][bi]])
                cur = dstb
                bi ^= 1
                stp *= 2
            if ti == 0:
                e1 = S.op("pool", lambda e: e.tensor_tensor(
                    out=cur[:, lo:lo + n], in0=cur[:, lo:lo + n], in1=INVC[:, g, 0:n], op=ALU.mult), [prev_ev])
            else:
                e1 = S.op("pool", lambda e: e.tensor_scalar(
                    out=cur[:, lo:lo + n], in0=cur[:, lo:lo + n], scalar1=1.0 / w, scalar2=None, op0=ALU.mult),
                    [prev_ev])
            ev_ = S.op("pool", lambda e: e.tensor_tensor(
                out=src[:, lo:lo + n], in0=cur[:, lo:lo + n], in1=src[:, lo:lo + n], op=ALU.subtract), [e1, e_ph])
            pool_st["s_free"] = [ev_, ev_]
            return ev_

        p_groups = [cg for cg in cfg.cgs if cg[0] == "p"]
        pst_ = {"g": 0, "k": 0, "last": None, "bfree": None}
        pool_ev = [None] * PT
        n_pblocks = len(p_groups) * DC

        def emit_pblocks(cnt):
            for _ in range(cnt):
                if pst_["g"] >= len(p_groups):
                    return
                gi_ = p_groups[pst_["g"]][3]
                k = pst_["k"]
                if k == 0:
                    pst_["bfree"] = sum((take_free(b) for b in pbanks), [])

                def mk(wb, k=k):
                    return [lambda e, j=j, k=k, wb=wb: e.matmul(
                        PS[:, pbanks[j], 0:T], wb[:, j * 128:(j + 1) * 128], H[:, k, 0:T],
                        start=(k == 0), stop=(k == DC - 1)) for j in range(4)]
                pst_["last"] = wblock(mk, ([hev[k]] + (pst_["bfree"] if k == 0 else [])))
                pst_["k"] += 1
                if pst_["k"] == DC:
                    for j in range(4):
                        pt = gi_ * 4 + j
                        ev = S.op("act", copy_fn("act", P[:, pt, 16:16 + T], PS[:, pbanks[j], 0:T]), [pst_["last"]])
                        bank_free[pbanks[j]] = [ev]
                        p_ev[pt] = ev
                    for j in range(4):
                        pt = gi_ * 4 + j
                        pool_ev[pt] = pool_tile(pt)
                    pst_["g"] += 1
                    pst_["k"] = 0

        scale = 128.0 ** -0.5
        ao_ev = []
        qbs = [qb for qb in range(T // 128) if qb * 128 >= c0]
        pairs = [(qb, h) for qb in qbs for h in range(NH)]
        set_free = [[] for _ in range(NSET)]
        pst = {}
        v_all = [v_ev[t_] for t_ in range(T // 128)]

        def att_a1(n_):
            qb, h = pairs[n_]
            g, i, qc = h // 4, n_ % NSET, qb * 128
            midx = 0 if (ti == 0 and qb == qbs[0]) else 1
            bsc = att_bank()
            e_sc = S.op("pe", lambda e: e.matmul(
                PS[:, bsc, 0:256], QR[:, h, qc:qc + 128], KR[:, g, qc:qc + 256], start=True, stop=True),
                [q_ev[h], k_ev[g]] + hist + take_free(bsc))
            e_s = S.op("dve", lambda e: e.scalar_tensor_tensor(
                out=SS[:, i, :], in0=PS[:, bsc, 0:256], scalar=scale, in1=MASK[:, midx, :],
                op0=ALU.mult, op1=ALU.add), [e_sc] + set_free[i] + const_evs)
            bank_free[bsc] = [e_s]
            e_mx = S.op("dve", lambda e: e.reduce_max(out=VEC[:, i, 0:1], in_=SS[:, i, :], axis=AX.X), [e_s])
            e_nm = S.op("dve", lambda e: e.tensor_scalar(
                out=VEC[:, i, 1:2], in0=VEC[:, i, 0:1], scalar1=SINKS[:, h:h + 1], scalar2=-1.0,
                op0=ALU.max, op1=ALU.mult), [e_mx])
            e_e = S.op("act", lambda e: e.activation(
                out=SS[:, i, :], in_=SS[:, i, :], func=AF.Exp, bias=VEC[:, i, 1:2], scale=1.0,
                accum_out=VEC[:, i, 2:3]), [e_nm])
            e_es = S.op("act", lambda e: e.activation(
                out=VEC[:, i, 3:4], in_=SINKS[:, h:h + 1], func=AF.Exp, bias=VEC[:, i, 1:2], scale=1.0),
                [e_nm, e_e])
            pst[n_] = dict(e_e=e_e, e_es=e_es)

        def att_a2(n_):
            i = n_ % NSET
            d = pst[n_]
            e_dn = S.op("dve", lambda e: e.tensor_tensor(
                out=VEC[:, i, 4:5], in0=VEC[:, i, 2:3], in1=VEC[:, i, 3:4], op=ALU.add), [d["e_e"], d["e_es"]])
            e_r = S.op("dve", lambda e: e.reciprocal(out=VEC[:, i, 5:6], in_=VEC[:, i, 4:5]), [e_dn])
            d["e_p"] = S.op("dve", lambda e: e.tensor_scalar(
                out=PB[:, i, :], in0=SS[:, i, :], scalar1=VEC[:, i, 5:6], scalar2=None, op0=ALU.mult), [e_r])

        def att_b(n_):
            i = n_ % NSET
            d = pst[n_]
            btr = att_bank()
            fr = take_free(btr)
            e_t = None
            for kb in range(2):
                e_t = S.op("pe", lambda e, kb=kb: e.transpose(
                    out=PSB[:, btr, kb * 128:(kb + 1) * 128], in_=PB[:, i, kb * 128:(kb + 1) * 128],
                    identity=IDENT[:, :]), [d["e_p"], ev_ident] + (fr if kb == 0 else []), signal=(kb == 1))
            d["e_pt"] = S.op("act", copy_fn("act", PTT[:, i, :], PSB[:, btr, 0:256]), [e_t])
            bank_free[btr] = [d["e_pt"]]

        def att_c(n_):
            qb, h = pairs[n_]
            g, i, qc = h // 4, n_ % NSET, qb * 128
            d = pst.pop(n_)
            bo = att_bank()
            fr = take_free(bo)
            e_o = None
            for kb in range(2):
                e_o = S.op("pe", lambda e, kb=kb: e.matmul(
                    PS[:, bo, 0:128], VT[:, qb + kb, g * 128:(g + 1) * 128], PTT[:, i, kb * 128:(kb + 1) * 128],
                    start=(kb == 0), stop=(kb == 1)),
                    [d["e_pt"]] + v_all + hist + (fr if kb == 0 else []), signal=(kb == 1))
            e_ao = S.op("dve", lambda e: e.tensor_copy(out=AO[:, h, qc:qc + 128], in_=PS[:, bo, 0:128]), [e_o])
            bank_free[bo] = [e_ao]
            set_free[i] = [e_ao, e_o]
            ao_ev.append(e_ao)

        npair = len(pairs)
        nsteps = npair + 3
        done_blocks = 0
        for step_ in range(nsteps):
            if step_ < npair:
                att_a1(step_)
            if 0 <= step_ - 1 < npair:
                att_a2(step_ - 1)
            if 0 <= step_ - 2 < npair:
                att_b(step_ - 2)
            if 0 <= step_ - 3 < npair:
                att_c(step_ - 3)
            tgt = (n_pblocks * (step_ + 1)) // nsteps
            emit_pblocks(tgt - done_blocks)
            done_blocks = tgt
        emit_pblocks(n_pblocks)
        last_in = pst_["last"]
        ncarry = {}
        pd_ev = [None] * PT
        for pt in range(PT):
            if pt // 4 == p_groups[-1][3]:
                eng = "pool"
            else:
                eng = "act" if pt % 2 == 0 else "dve"
            pd_ev[pt] = S.op(eng, copy_fn(eng, PD[:, pt, c0:T], P[:, pt, lo:lo + n]), [pool_ev[pt], last_in])
        st["nopool"] = False
        e_kc = S.op("pool", lambda e: e.tensor_copy(out=KR[:, :, 0:128], in_=KR[:, :, T:T + 128]),
                    [ev for ev in k_ev] + ao_ev)
        e_vc = S.op("pool", lambda e: e.tensor_copy(out=VT[:, 0, :], in_=VT[:, T // 128, :]),
                    [ev for ev in v_ev] + ao_ev)
        ncarry["hist"] = [e_kc, e_vc] + pool_st["ph"]
        ncarry["kr_free"] = [e_kc]
        ncarry["vt_free"] = [e_vc]
        pm_ev = [None] * PT
        for g in range(4):
            banks = alloc_banks(PGT)
            bfree = sum((take_free(b) for b in banks), [])
            last = None
            for c in range(PGT):
                def mk(wb, c=c, banks=banks, g=g):
                    return [lambda e, j=j, c=c, banks=banks, wb=wb, g=g: e.matmul(
                        PS[:, banks[j], c0:T], wb[:, j * 128:(j + 1) * 128], PD[:, g * PGT + c, c0:T],
                        start=(c == 0), stop=(c == PGT - 1)) for j in range(PGT)]
                last = wblock(mk, ([pd_ev[g * PGT + c]] + (bfree if c == 0 else [])))
            for j in range(PGT):
                pt = g * PGT + j
                ev = S.op("dve", lambda e, pt=pt, b=banks[j]: e.tensor_scalar(
                    out=PM[:, pt, c0:T], in0=PS[:, b, c0:T], scalar1=PSC[:, pt:pt + 1], scalar2=None,
                    op0=ALU.mult), [last] + pd_ev)
                bank_free[banks[j]] = [ev]
                pm_ev[pt] = ev
        if dbg == "mixdump":
            allev = ao_ev + pm_ev + pd_ev + [ev for ev in q_ev + k_ev + v_ev + p_ev if ev is not None]
            for (nm, ap_, shp, dt_) in (("d_qr", QR, [128, NH, T], BF16), ("d_kr", KR, [128, NKV, 128 + T], BF16),
                                        ("d_vt", VT, [128, 4, cfg.KVW], BF16), ("d_p", P, [128, PT, 16 + T], F32),
                                        ("d_pd", PD, [128, PT, T], BF16), ("d_pm", PM, [128, PT, T], BF16),
                                        ("d_ao", AO, [128, NH, T], BF16)):
                dd = nc.dram_tensor(nm, shp, dt_, kind="ExternalOutput").ap()
                S.dma(lambda e, dd=dd, ap_=ap_: e.dma_start(out=dd, in_=ap_), "dbgout", allev)
            return allev, ncarry
        yev = [None] * DC
        mix_ready = ao_ev + pm_ev
        for dg in range(DG):
            banks = alloc_banks(4)
            bfree = sum((take_free(b) for b in banks), [])
            last = None
            for c in range(DC):
                src_ = AO[:, c, c0:T] if c < NH else PM[:, c - NH, c0:T]
                def mk(wb, c=c, banks=banks, src_=src_):
                    return [lambda e, j=j, c=c, banks=banks, wb=wb, src_=src_: e.matmul(
                        PS[:, banks[j], c0:T], wb[:, j * 128:(j + 1) * 128], src_,
                        start=(c == 0), stop=(c == DC - 1)) for j in range(4)]
                last = wblock(mk, ((mix_ready if (dg == 0 and c == 0) else []) + (bfree if c == 0 else [])))
            for j in range(4):
                dt_ = dg * 4 + j
                eng = "act" if j % 2 == 0 else "dve"
                ev = S.op(eng, copy_fn(eng, Y[:, dt_, c0:T], PS[:, banks[j], c0:T]),
                          [last] + pd_ev + ao_ev[-1:])
                bank_free[banks[j]] = [ev]
                yev[dt_] = ev
        ncarry["rope_free"] = [ao_ev[-1]] if ao_ev else []
        xev = postnorm(3, c0, 1.0, yev)
        return xev, ncarry

    carry = {}
    NX = 4 if DC % 4 == 0 else 1
    step = DC // NX
    out_ev = [None] * NX
    for ti in range(cfg.NT):
        c0 = cfg.HALO if ti == 0 else 0
        xld = []
        for q in range(NX):
            xld.append(S.dma(lambda e, q=q, ti=ti: e.dma_start(
                out=X[:, q * step:(q + 1) * step, :], in_=xT[:, q * step:(q + 1) * step, ti * T:(ti + 1) * T]),
                "xin%d" % q, [out_ev[q]]))
        xw = [xld[c // step] for c in range(DC)]
        xev = ffn(0, 1, 0, xw)
        if dbg == "ffn1":
            break
        xev, carry = mixer(ti, c0, xev, carry)
        if dbg in ("mix", "mixdump"):
            break
        xev = ffn(4, 5, c0, xev)
        if dbg == "ffn2":
            break
        o0 = ti * T - cfg.HALO + c0
        for q in range(NX):
            out_ev[q] = S.dma(lambda e, q=q, o0=o0, c0=c0: e.dma_start(
                out=outT[:, q * step:(q + 1) * step, o0:o0 + T - c0], in_=X[:, q * step:(q + 1) * step, c0:T]),
                "xout%d" % q, xev[q * step:(q + 1) * step])
    if dbg:
        dbg_d = nc.dram_tensor("dbg", [128, DC, T], F32, kind="ExternalOutput").ap()
        out_ev = [S.dma(lambda e: e.dma_start(out=dbg_d, in_=X), "dbgout", xev)]
    S.wait_only("sync", out_ev)
    assert dbg or st["pe"] == TOTAL, (st, TOTAL)

    esem = {}
    for e_ in ("act", "dve", "pool", "pe"):
        esem[("eng", e_)] = es.enter_context(nc.semaphore("s_" + e_))
    for name in S.dma_cnt:
        esem[("dma", name)] = es.enter_context(nc.semaphore("d_" + name))

    def replay(engname, eng):
        waited = {}
        for (fn, waits, signal, dmasem) in S.ops[engname]:
            for w in waits:
                if w.eng == engname and engname == "pe":
                    continue
                if waited.get(w.sem, 0) >= w.val:
                    continue
                eng.wait_ge(esem[w.sem], w.val)
                waited[w.sem] = w.val
            if fn is None:
                continue
            inst = fn(eng)
            if dmasem is not None:
                inst.then_inc(esem[("dma", dmasem)], 16)
            elif signal:
                inst.then_inc(esem[("eng", engname)], 1)

    with nc.Block() as block:
        @block.sync
        def _(e):
            replay("sync", e)

        @block.scalar
        def _(e):
            replay("act", e)

        @block.vector
        def _(e):
            replay("dve", e)

        @block.gpsimd
        def _(e):
            replay("pool", e)

        @block.tensor
        def _(e):
            replay("pe", e)
    es.close()
    return nc


_CACHE = {}


def kernel(**inputs):
    cfg = Cfg()
    in_maps = build_core_inputs(cfg, inputs)
    if "nc" not in _CACHE:
        _CACHE["nc"] = build_program(cfg)
    res = run_bass_kernel_spmd(_CACHE["nc"], in_maps, core_ids=list(range(cfg.NCORE)))
    return assemble_output(cfg, res.results)
```
